# Optimizing a Trainium2 kernel written in Bass

```python
import jax, jax.numpy as jnp
from jax import lax
import numpy as np

D_MODEL = 4096
BATCH = 4
SEQ = 4096
DEPTH = 4

CHUNK = 64
CONV_CH = D_MODEL // 2
CONV_WIDTH = 31
GLA_HEADS = 4
GLA_DK = D_MODEL // 4 // GLA_HEADS
GLA_DV = D_MODEL // 2 // GLA_HEADS
GLA_LOWRANK = 16
GLA_TAU = 16.0
W_MIX = CONV_CH + GLA_HEADS * GLA_DV
D_FF = 4 * D_MODEL
N_MOD = 6
NORM_EPS = 1e-6
IN_COLS = 2 * CONV_CH + 2 * GLA_HEADS * GLA_DK + 2 * GLA_HEADS * GLA_DV + GLA_LOWRANK

kernel_name = 'hybrid_conformer_gla_adaln_trunk'


def _rmsnorm(x, g):
    xf = x.astype(jnp.float32)
    y = xf * lax.rsqrt(jnp.mean(xf * xf, axis=-1, keepdims=True) + NORM_EPS)
    return (y * g.astype(jnp.float32)).astype(x.dtype)


def _modulate(h, shift, scale):
    return h * (1 + scale[:, None, :]) + shift[:, None, :]


def _conformer_conv(u_val, u_gate, conv_w, conv_b, ln_g, ln_b):
    u = u_val * jax.nn.sigmoid(u_gate)
    u = jnp.pad(u, ((0, 0), (CONV_WIDTH - 1, 0), (0, 0)))
    y = lax.conv_general_dilated(u, conv_w[:, None, :].astype(u.dtype), (1,), 'VALID',
                                 dimension_numbers=('NWC', 'WIO', 'NWC'),
                                 feature_group_count=CONV_CH) + conv_b
    yf = y.astype(jnp.float32)
    mu = jnp.mean(yf, axis=-1, keepdims=True)
    var = jnp.mean(jnp.square(yf - mu), axis=-1, keepdims=True)
    yn = (yf - mu) * lax.rsqrt(var + NORM_EPS) * ln_g.astype(jnp.float32) + ln_b.astype(jnp.float32)
    return jax.nn.silu(yn).astype(u_val.dtype)


def _gla(q, k, v, g, a_lr, w_alpha, b_alpha, norm_g):
    f32 = jnp.float32
    b, s, _ = q.shape
    n = s // CHUNK

    def heads(t, d):
        return t.astype(f32).reshape(b, n, CHUNK, GLA_HEADS, d).transpose(0, 3, 1, 2, 4)

    log_a = jax.nn.log_sigmoid((a_lr @ w_alpha + b_alpha).astype(f32)) / GLA_TAU
    cum = jnp.cumsum(heads(log_a, GLA_DK), axis=3)
    qh = heads(q, GLA_DK) * GLA_DK ** -0.5
    kh = heads(k, GLA_DK)
    vh = heads(v, GLA_DV)
    q_dec = qh * jnp.exp(cum)
    k_in = kh * jnp.exp(-cum)
    k_st = kh * jnp.exp(cum[..., -1:, :] - cum)
    chunk_decay = jnp.exp(cum[..., -1, :])

    causal = jnp.tril(jnp.ones((CHUNK, CHUNK), dtype=bool))
    scores = jnp.einsum('bhnld,bhnmd->bhnlm', q_dec, k_in)
    scores = jnp.where(causal, scores, 0.0)
    o_intra = jnp.einsum('bhnlm,bhnmv->bhnlv', scores, vh)

    def step(state, inp):
        qd, ks, vc, dec = inp
        o = jnp.einsum('bhld,bhdv->bhlv', qd, state)
        state = dec[..., None] * state + jnp.einsum('bhld,bhlv->bhdv', ks, vc)
        return state, o

    state0 = jnp.zeros((b, GLA_HEADS, GLA_DK, GLA_DV), f32)
    xs = (jnp.moveaxis(q_dec, 2, 0), jnp.moveaxis(k_st, 2, 0),
          jnp.moveaxis(vh, 2, 0), jnp.moveaxis(chunk_decay, 2, 0))
    _, o_inter = lax.scan(step, state0, xs)
    o = o_intra + jnp.moveaxis(o_inter, 0, 2)
    o = o.transpose(0, 2, 3, 1, 4).reshape(b, s, GLA_HEADS, GLA_DV)
    o = o * lax.rsqrt(jnp.mean(o * o, axis=-1, keepdims=True) + NORM_EPS) * norm_g.astype(f32)
    o = o.reshape(b, s, GLA_HEADS * GLA_DV) * jax.nn.silu(g.astype(f32))
    return o.astype(q.dtype)


def setup_inputs(seed: int = 0) -> dict:
    key = jax.random.key(seed)
    ks = jax.random.split(key, 20)
    f32 = jnp.float32

    def nrm(k, shape, scale):
        return jax.random.normal(k, shape, f32) * scale

    x = nrm(ks[0], (BATCH, SEQ, D_MODEL), 1.0)
    c = nrm(ks[1], (BATCH, D_MODEL), 1.0)
    w_ada = nrm(ks[2], (D_MODEL, N_MOD * D_MODEL), 0.1 * D_MODEL ** -0.5)
    b_ada = nrm(ks[3], (N_MOD * D_MODEL,), 0.02)
    mod_table = nrm(ks[4], (DEPTH, N_MOD, D_MODEL), 0.1)
    mod_table = mod_table.at[:, 2::3, :].add(0.5)
    norm1_g = 1.0 + nrm(ks[5], (DEPTH, D_MODEL), 0.05)
    w_in = nrm(ks[6], (DEPTH, D_MODEL, IN_COLS), D_MODEL ** -0.5)
    conv_w = nrm(ks[7], (DEPTH, CONV_WIDTH, CONV_CH), CONV_WIDTH ** -0.5)
    conv_b = nrm(ks[8], (DEPTH, CONV_CH), 0.02)
    conv_ln_g = 1.0 + nrm(ks[9], (DEPTH, CONV_CH), 0.05)
    conv_ln_b = nrm(ks[10], (DEPTH, CONV_CH), 0.02)
    w_alpha = nrm(ks[11], (DEPTH, GLA_LOWRANK, GLA_HEADS * GLA_DK), GLA_LOWRANK ** -0.5)
    b_alpha = nrm(ks[12], (DEPTH, GLA_HEADS * GLA_DK), 0.1)
    gla_norm_g = 1.0 + nrm(ks[13], (DEPTH, GLA_DV), 0.05)
    w_out = nrm(ks[14], (DEPTH, W_MIX, D_MODEL), W_MIX ** -0.5)
    norm2_g = 1.0 + nrm(ks[15], (DEPTH, D_MODEL), 0.05)
    w_mlp1 = nrm(ks[16], (DEPTH, D_MODEL, D_FF), D_MODEL ** -0.5)
    w_mlp2 = nrm(ks[17], (DEPTH, D_FF, D_MODEL), D_FF ** -0.5)
    final_g = 1.0 + nrm(ks[18], (D_MODEL,), 0.05)
    return {'x': x, 'c': c, 'w_ada': w_ada, 'b_ada': b_ada, 'mod_table': mod_table,
            'norm1_g': norm1_g, 'w_in': w_in, 'conv_w': conv_w, 'conv_b': conv_b,
            'conv_ln_g': conv_ln_g, 'conv_ln_b': conv_ln_b, 'w_alpha': w_alpha,
            'b_alpha': b_alpha, 'gla_norm_g': gla_norm_g, 'w_out': w_out,
            'norm2_g': norm2_g, 'w_mlp1': w_mlp1, 'w_mlp2': w_mlp2, 'final_g': final_g}


def reference(x, c, w_ada, b_ada, mod_table, norm1_g, w_in, conv_w, conv_b, conv_ln_g,
              conv_ln_b, w_alpha, b_alpha, gla_norm_g, w_out, norm2_g, w_mlp1, w_mlp2, final_g):
    f32 = jnp.float32
    mod = (jax.nn.silu(c.astype(f32)) @ w_ada.astype(f32) + b_ada.astype(f32))
    mod = mod.reshape(c.shape[0], N_MOD, D_MODEL)
    sizes = (CONV_CH, CONV_CH, GLA_HEADS * GLA_DK, GLA_HEADS * GLA_DK,
             GLA_HEADS * GLA_DV, GLA_HEADS * GLA_DV)
    offsets = np.cumsum(sizes).tolist()
    for l in range(DEPTH):
        m = (mod + mod_table[l].astype(f32)).astype(x.dtype)
        h = _modulate(_rmsnorm(x, norm1_g[l]), m[:, 0], m[:, 1])
        z = h @ w_in[l]
        a_val, a_gate, q, k, v, g, a_lr = jnp.split(z, offsets, axis=-1)
        y_a = _conformer_conv(a_val, a_gate, conv_w[l], conv_b[l], conv_ln_g[l], conv_ln_b[l])
        y_b = _gla(q, k, v, g, a_lr, w_alpha[l], b_alpha[l], gla_norm_g[l])
        y = jnp.concatenate([y_a, y_b], axis=-1) @ w_out[l]
        x = x + m[:, 2, None, :] * y
        h = _modulate(_rmsnorm(x, norm2_g[l]), m[:, 3], m[:, 4])
        f = jnp.square(jax.nn.relu(h @ w_mlp1[l])) @ w_mlp2[l]
        x = x + m[:, 5, None, :] * f
    return _rmsnorm(x, final_g)
```

```python
import numpy as np
import concourse.bass as bass
import concourse.mybir as mybir
from concourse.bass_utils import run_bass_kernel_spmd

F32 = mybir.dt.float32
BF16 = mybir.dt.bfloat16
ALU = mybir.AluOpType
AF = mybir.ActivationFunctionType
P = 128
CONV_W = 31
HALO = CONV_W - 1
EPS = 1e-6
TAU = 16.0

FULL_CFG = dict(NT=4096, DEPTH=4, D=4096, CC=2048, H=4, DK=256, DV=512, LR=16, DFF=16384,
                TS=256, FB=1024, NCORES=4)


class Sched:
    ENGS = ("tensor", "vector", "scalar", "gpsimd", "sync")

    def __init__(self):
        self.streams = {e: [] for e in self.ENGS}
        self.cnt = {}
        self.waited = {e: {} for e in self.ENGS}
        self.lastw = {}
        self.readers = {}
        self.epoch = {}
        self.last_dma = {}
        self.marks = []

    LIMIT = 30000
    CSCALE = 1
    DSCALE = 1

    def op(self, eng, fn, reads=(), writes=(), dma_sem=None):
        needs = {}

        def need(s, v):
            if needs.get(s, 0) < v:
                needs[s] = v
        for b in reads:
            lw = self.lastw.get(b)
            if lw:
                need(*lw)
        for b in writes:
            lw = self.lastw.get(b)
            if lw:
                need(*lw)
            for s, v in self.readers.get(b, {}).items():
                need(s, v)
        if dma_sem is None:
            base, inc = "c_" + eng, self.CSCALE
        else:
            base, inc = dma_sem, 16 * self.DSCALE
            ld = self.last_dma.get(base)
            if ld:
                need(*ld)
        ep = self.epoch.get(base, 0)
        sem = "%s_e%d" % (base, ep)
        if self.cnt.get(sem, 0) + inc > self.LIMIT:
            ep += 1
            self.epoch[base] = ep
            sem = "%s_e%d" % (base, ep)
        wl = []
        wd = self.waited[eng]
        for s, v in needs.items():
            if wd.get(s, 0) < v:
                wd[s] = v
                wl.append((s, v))
        val = self.cnt.get(sem, 0) + inc
        self.cnt[sem] = val
        if dma_sem is not None:
            self.last_dma[base] = (sem, val)
        self.streams[eng].append((wl, fn, sem, inc))
        for b in reads:
            r = self.readers.setdefault(b, {})
            if r.get(sem, 0) < val:
                r[sem] = val
        for b in writes:
            self.lastw[b] = (sem, val)
            self.readers[b] = {}

    def mark(self):
        self.marks.append({e: len(self.streams[e]) for e in self.ENGS})

    def final_wait(self, eng, sems):
        wl = [self.last_dma[s] for s in sems if s in self.last_dma]
        self.streams[eng].append((wl, None, None, 0))


def build_program(cfg):
    NT, DEPTH, D, CC, H, DK, DV, LR, DFF, TS, FB = (cfg[k] for k in
        ("NT", "DEPTH", "D", "CC", "H", "DK", "DV", "LR", "DFF", "TS", "FB"))
    KT = D // P
    CT = CC // P
    QW = H * DK
    QT = QW // P
    DKT = DK // P
    VW = H * DV
    VT = DV // P
    MIX = CC + VW
    MT = MIX // P
    IN_COLS = 2 * CC + 2 * QW + 2 * VW + LR
    O_AV, O_AG, O_Q, O_K, O_V, O_G, O_LR = 0, CC, 2 * CC, 2 * CC + QW, 2 * CC + 2 * QW, 2 * CC + 2 * QW + VW, 2 * CC + 2 * QW + 2 * VW
    NST = NT // TS
    FINAL = cfg.get("FINAL", True)
    NTT = TS // P
    NFB = DFF // FB
    FBT = FB // P
    WSLOT = 4096
    NSLOT = 2
    assert DV == 512 and TS % P == 0 and TS <= 512 and KT * P <= WSLOT

    nc = bass.Bass("TRN2", target_bir_lowering=False)

    def din(name, shape, dt=F32):
        return nc.dram_tensor(name, list(shape), dt, kind="ExternalInput")
    xT_in = din("xT", [D, NT])
    scT_in = din("scT", [P, KT])
    w_ada = din("w_ada", [D, 6 * D])
    b_adaT = din("b_adaT", [P, 6 * KT])
    modtabT = din("modtabT", [DEPTH, P, 6 * KT])
    n1gT = din("n1gT", [DEPTH, P, KT])
    n2gT = din("n2gT", [DEPTH, P, KT])
    fgT = din("fgT", [P, KT])
    w_in = din("w_in", [DEPTH, D, IN_COLS])
    w_out = din("w_out", [DEPTH, MIX, D])
    w_m1 = din("w_m1", [DEPTH, D, DFF])
    w_m2 = din("w_m2", [DEPTH, DFF, D])
    convwT = din("convwT", [DEPTH, P, CT * CONV_W])
    convbT = din("convbT", [DEPTH, P, CT])
    lngT = din("lngT", [DEPTH, P, CT])
    lnbT = din("lnbT", [DEPTH, P, CT])
    walpha = din("walpha", [DEPTH, 32, QW])
    gngbc = din("gngbc", [DEPTH, P, DV])
    cst = din("cst", [P, 5 * P])
    outT = nc.dram_tensor("outT", [D, NT], F32, kind="ExternalOutput")
    xs = nc.dram_tensor("xs", [D, NT], F32)

    S = Sched()
    tiles = {}
    ctx = []

    def sb(name, shape, dt=F32):
        t = nc.sbuf_tensor(name, list(shape), dt)
        ctx.append(t)
        tiles[name] = t.__enter__()
        return tiles[name]

    x_sb = sb("x_sb", [P, KT, TS])
    hnT = sb("hnT", [P, KT, TS], BF16)
    wsl = [sb("wsl%d" % i, [P, WSLOT], BF16) for i in range(NSLOT)]
    stg = sb("stg", [P, WSLOT])
    u_ext = sb("u_ext", [P, CT, HALO + TS], BF16)
    assert CT == 2 * QT
    EE = sb("EE", [P, 2 * QT, TS])
    Ep = EE[:, 0:QT, :]
    En = EE[:, QT:2 * QT, :]
    yc = EE
    dec = sb("dec", [P, QT, NTT])
    lsp = sb("lsp", [P, NTT, QW])
    qdT = sb("qdT", [P, QT, TS], BF16)
    kiT = sb("kiT", [P, QT, TS], BF16)
    kst = sb("kst", [P, NTT, QW], BF16)
    vtk = sb("vtk", [P, NTT, VW], BF16)
    sgt = sb("sgt", [P, NTT, VW], BF16)
    St = sb("St", [P, H * DKT, DV])
    Sb = sb("Sb", [P, H * DKT, DV], BF16)
    h1T = sb("h1T", [P, FBT, TS], BF16)
    rs = sb("rs", [P, 4, TS])
    tmp = sb("tmp", [P, 4, 512])
    tmpb = sb("tmpb", [P, 2, 512], BF16)
    scm = sb("scm", [P, 2, P], BF16)
    modc = sb("modc", [P, 6 * KT])
    modl = sb("modl", [P, 6 * KT])
    par = sb("par", [P, 4 * KT + 4 * CT + CT * CONV_W])
    gng = sb("gng", [P, DV])
    wal = sb("wal", [32, QW])
    cs = sb("cs", [P, 5 * P])
    identb = sb("identb", [P, P], BF16)
    masku = sb("masku", [P, P], BF16)
    alr = sb("alr", [32, TS])
    sct = sb("sct", [P, KT])
    sctb = sb("sctb", [P, KT], BF16)
    col = sb("col", [P, 8])
    ssq = sb("ssq", [P, 8])

    fgt = sb("fgt", [P, KT])
    pst = []
    NPS = 7
    for i in range(NPS):
        t = nc.psum_tensor("ps%d" % i, [P, 512], F32)
        ctx.append(t)
        pst.append(t.__enter__())
    t = nc.psum_tensor("ptb", [P, 1024], BF16)
    ctx.append(t)
    ptb = t.__enter__()
    ptk = "ptb"
    ps_state = {"i": 0}

    def nextps():
        i = ps_state["i"]
        ps_state["i"] = (i + 1) % NPS
        return pst[i], ("ps", i)

    ws_state = {"i": 0}

    def wload(src3, a, b):
        i = ws_state["i"]
        ws_state["i"] = (i + 1) % NSLOT
        dst = wsl[i][:, 0:a * b].rearrange("p (a b) -> p a b", b=b)
        sv = stg[:, 0:a * b].rearrange("p (a b) -> p a b", b=b)
        S.op("sync", lambda e, sv=sv, src3=src3: e.dma_start(out=sv, in_=src3),
             writes=["stg"], dma_sem="dstg")
        S.op("gpsimd", lambda e, dst=dst, sv=sv: e.tensor_copy(dst, sv), reads=["stg"], writes=[("w", i)])
        return dst, ("w", i)

    ones_f = cs[:, 0:P]
    triN = cs[:, P:2 * P]
    sutriN = cs[:, 2 * P:3 * P]
    c_one = col[:, 0:1]
    c_zero = col[:, 1:2]
    c_eps = col[:, 2:3]

    S.op("sync", lambda e: e.dma_start(out=cs[:], in_=cst[:, :]), writes=["cs"], dma_sem="dpar")
    S.op("sync", lambda e: e.dma_start(out=sct[:], in_=scT_in[:, :]), writes=["sct"], dma_sem="dpar")
    S.op("sync", lambda e: e.dma_start(out=modc[:], in_=b_adaT[:, :]), writes=["modc"], dma_sem="dpar")
    S.op("sync", lambda e: e.dma_start(out=fgt[:], in_=fgT[:, :]), writes=["fgt"], dma_sem="dpar")
    S.op("vector", lambda e: e.tensor_copy(identb[:], cs[:, 3 * P:4 * P]), reads=["cs"], writes=["identb"])
    S.op("vector", lambda e: e.tensor_copy(masku[:], cs[:, 4 * P:5 * P]), reads=["cs"], writes=["masku"])
    S.op("vector", lambda e: e.memset(col[:, 0:1], 1.0), writes=["col"])
    S.op("vector", lambda e: e.memset(col[:, 1:2], 0.0), writes=["col"])
    S.op("vector", lambda e: e.memset(col[:, 2:3], EPS), writes=["col"])
    S.op("vector", lambda e: e.memset(alr[:], 1.0), writes=["alr"])
    S.op("scalar", lambda e: e.activation(sctb[:], sct[:], AF.Silu, bias=c_zero, scale=1.0),
         reads=["sct", "col"], writes=["sctb"])
    NJ = 6 * KT
    ps_mod, k_mod = nextps()
    for j0 in range(0, NJ, 1):
        wv, wk = wload(w_ada[:, j0 * P:(j0 + 1) * P].rearrange("(kt p) c -> p kt c", p=P), KT, P)

        def f(e, wv=wv, j0=j0):
            ins = None
            for jj in range(1):
                for kt in range(KT):
                    ins = e.matmul(ps_mod[:, j0 + jj:j0 + jj + 1], wv[:, kt, jj * P:(jj + 1) * P],
                                   sctb[:, kt:kt + 1], start=(kt == 0), stop=(kt == KT - 1))
            return ins
        S.op("tensor", f, reads=[wk, "sctb"], writes=[k_mod])
    S.op("vector", lambda e: e.tensor_tensor(modc[:], modc[:], ps_mod[:, 0:NJ], ALU.add),
         reads=[k_mod, "modc"], writes=["modc"])

    A1 = par[:, 0:KT]
    A2 = par[:, KT:2 * KT]
    n1g = par[:, 2 * KT:3 * KT]
    n2g = par[:, 3 * KT:4 * KT]
    o0 = 4 * KT
    convb = par[:, o0:o0 + CT]
    lng = par[:, o0 + CT:o0 + 2 * CT]
    lnb = par[:, o0 + 2 * CT:o0 + 3 * CT]
    cw = par[:, o0 + 4 * CT:o0 + 4 * CT + CT * CONV_W]

    def mcol(six, kt):
        return modl[:, six * KT + kt:six * KT + kt + 1]

    def norm_to_hnT(Acols, six_shift):
        ps, pk = nextps()
        for kt in range(KT):
            S.op("scalar", lambda e, kt=kt: e.activation(tmp[:, kt % 2, 0:TS], x_sb[:, kt, :], AF.Square,
                                                         bias=c_zero, scale=1.0),
                 reads=[("x", kt), "col"], writes=[("tmp", kt % 2)])
            S.op("tensor", lambda e, kt=kt: e.matmul(ps[:, 0:TS], ones_f, tmp[:, kt % 2, 0:TS],
                                                     start=(kt == 0), stop=(kt == KT - 1)),
                 reads=[("tmp", kt % 2), "cs"], writes=[pk])
        S.op("scalar", lambda e: e.activation(rs[:, 0, :], ps[:, 0:TS], AF.Sqrt, bias=c_eps, scale=1.0 / D),
             reads=[pk, "col"], writes=[("rs", 0)])
        S.op("vector", lambda e: e.reciprocal(rs[:, 0, :], rs[:, 0, :]),
             reads=[("rs", 0)], writes=[("rs", 0)])
        for kt in range(KT):
            S.op("vector", lambda e, kt=kt: e.scalar_tensor_tensor(tmp[:, 2 + kt % 2, 0:TS], x_sb[:, kt, :],
                                                                  Acols[:, kt:kt + 1], rs[:, 0, :], ALU.mult, ALU.mult),
                 reads=[("x", kt), ("rs", 0), "par"], writes=[("tmp", 2 + kt % 2)])
            S.op("scalar", lambda e, kt=kt: e.activation(hnT[:, kt, :], tmp[:, 2 + kt % 2, 0:TS], AF.Identity,
                                                         bias=mcol(six_shift, kt), scale=1.0),
                 reads=[("tmp", 2 + kt % 2), "modl"], writes=[("hn", kt)])

    HN_ALL = [("hn", kt) for kt in range(KT)]

    def gemm_fm(wmat2d, c0, ntile, rhs3, rhs_keys, nk, evac):
        j = 0
        while j < ntile:
            nb = min(WSLOT // (nk * P), ntile - j)
            wv, wk = wload(wmat2d[:, c0 + j * P:c0 + (j + nb) * P].rearrange("(kt p) c -> p kt c", p=P), nk, nb * P)
            for jj in range(nb):
                ps, pk = nextps()

                def f(e, wv=wv, jj=jj, ps=ps):
                    ins = None
                    for kt in range(nk):
                        ins = e.matmul(ps[:, 0:TS], wv[:, kt, jj * P:(jj + 1) * P], rhs3[:, kt, :],
                                       start=(kt == 0), stop=(kt == nk - 1))
                    return ins
                S.op("tensor", f, reads=[wk] + rhs_keys, writes=[pk])
                evac(j + jj, ps, pk)
            j += nb

    def gemm_tm(wmat2d, c0, ncols, evac):
        for cb in range(0, ncols, P):
            wv, wk = wload(wmat2d[:, c0 + cb:c0 + cb + P].rearrange("(kt p) c -> p kt c", p=P), KT, P)
            for tt in range(NTT):
                ps, pk = nextps()

                def f(e, wv=wv, tt=tt, ps=ps):
                    ins = None
                    for kt in range(KT):
                        ins = e.matmul(ps[:, 0:P], hnT[:, kt, tt * P:(tt + 1) * P], wv[:, kt, :],
                                       start=(kt == 0), stop=(kt == KT - 1))
                    return ins
                S.op("tensor", f, reads=[wk] + HN_ALL, writes=[pk])
                evac(tt, cb, ps, pk)

    for l in range(DEPTH):
        last = (l == DEPTH - 1)
        S.op("sync", lambda e, l=l: e.dma_start(out=modl[:], in_=modtabT[l, :, :]), writes=["modl"], dma_sem="dpar")
        S.op("sync", lambda e, l=l: e.dma_start(out=par[:, 2 * KT:3 * KT], in_=n1gT[l, :, :]), writes=["par"], dma_sem="dpar")
        S.op("sync", lambda e, l=l: e.dma_start(out=par[:, 3 * KT:4 * KT], in_=n2gT[l, :, :]), writes=["par"], dma_sem="dpar")
        S.op("sync", lambda e, l=l: e.dma_start(out=par[:, o0:o0 + CT], in_=convbT[l, :, :]), writes=["par"], dma_sem="dpar")
        S.op("sync", lambda e, l=l: e.dma_start(out=par[:, o0 + CT:o0 + 2 * CT], in_=lngT[l, :, :]), writes=["par"], dma_sem="dpar")
        S.op("sync", lambda e, l=l: e.dma_start(out=par[:, o0 + 2 * CT:o0 + 3 * CT], in_=lnbT[l, :, :]), writes=["par"], dma_sem="dpar")
        S.op("sync", lambda e, l=l: e.dma_start(out=par[:, o0 + 4 * CT:o0 + 4 * CT + CT * CONV_W], in_=convwT[l, :, :]), writes=["par"], dma_sem="dpar")
        S.op("sync", lambda e, l=l: e.dma_start(out=gng[:], in_=gngbc[l, :, :]), writes=["gng"], dma_sem="dpar")
        S.op("sync", lambda e, l=l: e.dma_start(out=wal[:], in_=walpha[l, :, :]), writes=["wal"], dma_sem="dpar")
        S.op("vector", lambda e: e.tensor_tensor(modl[:], modl[:], modc[:], ALU.add), reads=["modl", "modc"], writes=["modl"])
        S.op("vector", lambda e: e.scalar_tensor_tensor(par[:, 0:KT], modl[:, KT:2 * KT], 1.0, par[:, 2 * KT:3 * KT], ALU.add, ALU.mult),
             reads=["modl", "par"], writes=["par"])
        S.op("vector", lambda e: e.scalar_tensor_tensor(par[:, KT:2 * KT], modl[:, 4 * KT:5 * KT], 1.0, par[:, 3 * KT:4 * KT], ALU.add, ALU.mult),
             reads=["modl", "par"], writes=["par"])
        S.op("vector", lambda e: e.memset(St[:], 0.0), writes=["St"])
        S.op("vector", lambda e: e.memset(Sb[:], 0.0), writes=["Sb"])
        S.op("vector", lambda e: e.memset(u_ext[:, :, 0:HALO], 0.0), writes=[("u", j) for j in range(CT)])

        xsrc = xT_in if l == 0 else xs
        for st in range(NST):
            t0 = st * TS
            S.mark()
            S.op("sync", lambda e, xsrc=xsrc, t0=t0: e.dma_start(
                out=x_sb[:], in_=xsrc[:, t0:t0 + TS].rearrange("(kt p) t -> p kt t", p=P)),
                reads=["xdram"], writes=[("x", kt) for kt in range(KT)], dma_sem="dx")
            norm_to_hnT(A1, 0)
            win = w_in[l]
            wv, wk = wload(win[:, O_LR:O_LR + LR].rearrange("(kt p) c -> p kt c", p=P), KT, LR)
            ps, pk = nextps()

            def f(e, wv=wv, ps=ps):
                ins = None
                for kt in range(KT):
                    ins = e.matmul(ps[0:LR, 0:TS], wv[:, kt, :], hnT[:, kt, :], start=(kt == 0), stop=(kt == KT - 1))
                return ins
            S.op("tensor", f, reads=[wk] + HN_ALL, writes=[pk])
            S.op("vector", lambda e, ps=ps: e.tensor_copy(alr[0:LR, :], ps[0:LR, 0:TS]), reads=[pk], writes=["alr"])
            for tt in range(NTT):
                for hf in range(QW // 512):
                    ps, pk = nextps()
                    S.op("tensor", lambda e, ps=ps, tt=tt, hf=hf: e.matmul(
                        ps[:, :], alr[0:LR + 1, tt * P:(tt + 1) * P], wal[0:LR + 1, hf * 512:(hf + 1) * 512], start=True, stop=True),
                        reads=["alr", "wal"], writes=[pk])
                    S.op("scalar", lambda e, ps=ps, tt=tt, hf=hf: e.activation(
                        lsp[:, tt, hf * 512:(hf + 1) * 512], ps[:, :], AF.Exp, bias=c_zero, scale=-1.0),
                        reads=[pk, "col"], writes=[("lsp", tt, hf)])
                    S.op("scalar", lambda e, tt=tt, hf=hf: e.activation(
                        lsp[:, tt, hf * 512:(hf + 1) * 512], lsp[:, tt, hf * 512:(hf + 1) * 512], AF.Ln, bias=c_one, scale=1.0),
                        reads=[("lsp", tt, hf), "col"], writes=[("lsp", tt, hf)])
            for dt in range(QT):
                ps, pk = nextps()

                def f(e, ps=ps, dt=dt):
                    ins = None
                    for tt in range(NTT):
                        ins = e.matmul(ps[:, tt * P:(tt + 1) * P], lsp[:, tt, dt * P:(dt + 1) * P], triN, start=True, stop=True)
                    return ins
                S.op("tensor", f, reads=[("lsp", tt, (dt * P) // 512) for tt in range(NTT)] + ["cs"], writes=[pk])
                S.op("scalar", lambda e, ps=ps, dt=dt: e.activation(Ep[:, dt, :], ps[:, 0:TS], AF.Exp, bias=c_zero, scale=1.0),
                     reads=[pk, "col"], writes=[("EE", dt)])
                S.op("scalar", lambda e, ps=ps, dt=dt: e.activation(En[:, dt, :], ps[:, 0:TS], AF.Exp, bias=c_zero, scale=-1.0),
                     reads=[pk, "col"], writes=[("EE", QT + dt)])
                for tt in range(NTT):
                    S.op("vector", lambda e, dt=dt, tt=tt: e.tensor_copy(dec[:, dt, tt:tt + 1], Ep[:, dt, tt * P + P - 1:tt * P + P]),
                         reads=[("EE", dt)], writes=[("dec", dt)])
            for tt in range(NTT):
                for hf in range(QW // 512):
                    ps, pk = nextps()
                    S.op("tensor", lambda e, ps=ps, tt=tt, hf=hf: e.matmul(
                        ps[:, :], sutriN, lsp[:, tt, hf * 512:(hf + 1) * 512], start=True, stop=True),
                        reads=[("lsp", tt, hf), "cs"], writes=[pk])
                    S.op("scalar", lambda e, ps=ps, tt=tt, hf=hf: e.activation(
                        lsp[:, tt, hf * 512:(hf + 1) * 512], ps[:, :], AF.Exp, bias=c_zero, scale=1.0),
                        reads=[pk, "col"], writes=[("lsp", tt, hf)])

            def ev_q(j, ps, pk):
                S.op("vector", lambda e: e.scalar_tensor_tensor(qdT[:, j, :], ps[:, 0:TS], float(DK) ** -0.5, Ep[:, j, :], ALU.mult, ALU.mult),
                     reads=[pk, ("EE", j)], writes=[("qd", j)])
            gemm_fm(win, O_Q, QT, hnT, HN_ALL, KT, ev_q)

            def ev_k(j, ps, pk):
                S.op("vector", lambda e: e.tensor_tensor(kiT[:, j, :], ps[:, 0:TS], En[:, j, :], ALU.mult),
                     reads=[pk, ("EE", QT + j)], writes=[("ki", j)])
            gemm_fm(win, O_K, QT, hnT, HN_ALL, KT, ev_k)

            def ev_kt(tt, cb, ps, pk):
                S.op("vector", lambda e: e.tensor_tensor(kst[:, tt, cb:cb + P], ps[:, 0:P], lsp[:, tt, cb:cb + P], ALU.mult),
                     reads=[pk, ("lsp", tt, cb // 512)], writes=[("kst", tt)])
            gemm_tm(win, O_K, QW, ev_kt)

            def ev_v(tt, cb, ps, pk):
                S.op("scalar", lambda e: e.activation(vtk[:, tt, cb:cb + P], ps[:, 0:P], AF.Identity, bias=c_zero, scale=1.0),
                     reads=[pk, "col"], writes=[("v", tt)])
            gemm_tm(win, O_V, VW, ev_v)

            def ev_g(tt, cb, ps, pk):
                S.op("scalar", lambda e: e.activation(sgt[:, tt, cb:cb + P], ps[:, 0:P], AF.Silu, bias=c_zero, scale=1.0),
                     reads=[pk, "col"], writes=[("sg", tt)])
            gemm_tm(win, O_G, VW, ev_g)

            for j in range(CT):
                wv, wk = wload(win[:, O_AV + j * P:O_AV + (j + 1) * P].rearrange("(kt p) c -> p kt c", p=P), KT, P)
                wg, wgk = wload(win[:, O_AG + j * P:O_AG + (j + 1) * P].rearrange("(kt p) c -> p kt c", p=P), KT, P)
                psa, pka = nextps()
                psg, pkg = nextps()
                for (wv_, wk_, ps_, pk_) in ((wv, wk, psa, pka), (wg, wgk, psg, pkg)):
                    def f(e, wv_=wv_, ps_=ps_):
                        ins = None
                        for kt in range(KT):
                            ins = e.matmul(ps_[:, 0:TS], wv_[:, kt, :], hnT[:, kt, :], start=(kt == 0), stop=(kt == KT - 1))
                        return ins
                    S.op("tensor", f, reads=[wk_] + HN_ALL, writes=[pk_])
                S.op("scalar", lambda e, psg=psg, j=j: e.activation(tmp[:, j % 2, 0:TS], psg[:, 0:TS], AF.Sigmoid, bias=c_zero, scale=1.0),
                     reads=[pkg, "col"], writes=[("tmp", j % 2)])
                S.op("vector", lambda e, psa=psa, j=j: e.tensor_tensor(u_ext[:, j, HALO:HALO + TS], psa[:, 0:TS], tmp[:, j % 2, 0:TS], ALU.mult),
                     reads=[pka, ("tmp", j % 2)], writes=[("u", j)])

            for j in range(CT):
                S.op("vector", lambda e, j=j: e.tensor_scalar(yc[:, j, :], u_ext[:, j, 0:TS], cw[:, j * CONV_W:j * CONV_W + 1],
                                                              convb[:, j:j + 1], ALU.mult, ALU.add),
                     reads=[("u", j), "par"], writes=[("EE", j)])

                for tp in range(1, CONV_W):
                    S.op("vector", lambda e, j=j, tp=tp: e.scalar_tensor_tensor(
                        yc[:, j, :], u_ext[:, j, tp:tp + TS], cw[:, j * CONV_W + tp:j * CONV_W + tp + 1], yc[:, j, :], ALU.mult, ALU.add),
                        reads=[("u", j), "par", ("EE", j)], writes=[("EE", j)])
                S.op("scalar", lambda e, j=j: e.activation(u_ext[:, j, 0:HALO], u_ext[:, j, TS:TS + HALO], AF.Identity, bias=c_zero, scale=1.0),
                     reads=[("u", j), "col"], writes=[("u", j)])
            ps1, pk1 = nextps()
            ps2, pk2 = nextps()
            for j in range(CT):
                S.op("tensor", lambda e, j=j: e.matmul(ps1[:, 0:TS], ones_f, yc[:, j, :], start=(j == 0), stop=(j == CT - 1)),
                     reads=[("EE", j), "cs"], writes=[pk1])
                S.op("scalar", lambda e, j=j: e.activation(tmp[:, j % 2, 0:TS], yc[:, j, :], AF.Square, bias=c_zero, scale=1.0),
                     reads=[("EE", j), "col"], writes=[("tmp", j % 2)])
                S.op("tensor", lambda e, j=j: e.matmul(ps2[:, 0:TS], ones_f, tmp[:, j % 2, 0:TS], start=(j == 0), stop=(j == CT - 1)),
                     reads=[("tmp", j % 2), "cs"], writes=[pk2])
            S.op("vector", lambda e: e.tensor_scalar(rs[:, 1, :], ps1[:, 0:TS], 1.0 / CC, 0.0, ALU.mult, ALU.add), reads=[pk1], writes=[("rs", 1)])
            S.op("vector", lambda e: e.tensor_tensor(rs[:, 3, :], rs[:, 1, :], rs[:, 1, :], ALU.mult), reads=[("rs", 1)], writes=[("rs", 3)])
            S.op("vector", lambda e: e.scalar_tensor_tensor(rs[:, 2, :], ps2[:, 0:TS], 1.0 / CC, rs[:, 3, :], ALU.mult, ALU.subtract),
                 reads=[pk2, ("rs", 3)], writes=[("rs", 2)])
            S.op("scalar", lambda e: e.activation(rs[:, 2, :], rs[:, 2, :], AF.Sqrt, bias=c_eps, scale=1.0), reads=[("rs", 2), "col"], writes=[("rs", 2)])
            S.op("vector", lambda e: e.reciprocal(rs[:, 2, :], rs[:, 2, :]), reads=[("rs", 2)], writes=[("rs", 2)])
            for j in range(CT):
                S.op("vector", lambda e, j=j: e.tensor_tensor(tmp[:, 2 + j % 2, 0:TS], yc[:, j, :], rs[:, 1, :], ALU.subtract),
                     reads=[("EE", j), ("rs", 1)], writes=[("tmp", 2 + j % 2)])
                S.op("vector", lambda e, j=j: e.tensor_tensor(tmp[:, 2 + j % 2, 0:TS], tmp[:, 2 + j % 2, 0:TS], rs[:, 2, :], ALU.mult),
                     reads=[("tmp", 2 + j % 2), ("rs", 2)], writes=[("tmp", 2 + j % 2)])
                S.op("scalar", lambda e, j=j: e.activation(hnT[:, j, :], tmp[:, 2 + j % 2, 0:TS], AF.Silu, bias=lnb[:, j:j + 1], scale=lng[:, j:j + 1]),
                     reads=[("tmp", 2 + j % 2), "par"], writes=[("hn", j)])

            for tt in range(NTT):
                tsl = slice(tt * P, (tt + 1) * P)
                for h in range(H):
                    ps, pk = nextps()

                    def f(e, ps=ps, h=h, tsl=tsl):
                        ins = None
                        for dt in range(DKT):
                            ins = e.matmul(ps[:, 0:P], kiT[:, h * DKT + dt, tsl], qdT[:, h * DKT + dt, tsl], start=(dt == 0), stop=(dt == DKT - 1))
                        return ins
                    S.op("tensor", f, reads=[("ki", h * DKT + dt) for dt in range(DKT)] + [("qd", h * DKT + dt) for dt in range(DKT)], writes=[pk])
                    S.op("vector", lambda e, ps=ps, h=h: e.tensor_tensor(scm[:, h % 2, :], ps[:, 0:P], masku[:], ALU.mult),
                         reads=[pk, "masku"], writes=[("scm", h % 2)])
                    pso, pko = nextps()

                    def f(e, pso=pso, h=h, tt=tt, tsl=tsl):
                        e.matmul(pso[:, :], scm[:, h % 2, :], vtk[:, tt, h * DV:(h + 1) * DV], start=True, stop=False)
                        ins = None
                        for dt in range(DKT):
                            ins = e.matmul(pso[:, :], qdT[:, h * DKT + dt, tsl], Sb[:, h * DKT + dt, :], start=False, stop=(dt == DKT - 1))
                        return ins
                    S.op("tensor", f, reads=[("scm", h % 2), ("v", tt), ("Sb", h)] + [("qd", h * DKT + dt) for dt in range(DKT)], writes=[pko])
                    for dt in range(DKT):
                        pss, pks = nextps()
                        S.op("tensor", lambda e, pss=pss, h=h, dt=dt, tt=tt: e.matmul(
                            pss[:, :], kst[:, tt, (h * DKT + dt) * P:(h * DKT + dt + 1) * P], vtk[:, tt, h * DV:(h + 1) * DV], start=True, stop=True),
                            reads=[("kst", tt), ("v", tt)], writes=[pks])
                        S.op("vector", lambda e, pss=pss, h=h, dt=dt, tt=tt: e.scalar_tensor_tensor(
                            St[:, h * DKT + dt, :], St[:, h * DKT + dt, :], dec[:, h * DKT + dt, tt:tt + 1], pss[:, :], ALU.mult, ALU.add),
                            reads=[pks, ("dec", h * DKT + dt), ("St", h, dt)], writes=[("St", h, dt)])
                    S.op("scalar", lambda e, pso=pso, h=h: e.activation(tmp[:, h % 2, :], pso[:, :], AF.Square, bias=c_zero, scale=1.0,
                                                                        accum_out=ssq[:, h:h + 1]),
                         reads=[pko, "col"], writes=[("tmp", h % 2), ("ssq", h)])
                    S.op("scalar", lambda e, h=h: e.activation(ssq[:, 4 + h:5 + h], ssq[:, h:h + 1], AF.Sqrt, bias=c_eps, scale=1.0 / DV),
                         reads=[("ssq", h), "col"], writes=[("ssq", 4 + h)])
                    S.op("vector", lambda e, h=h: e.reciprocal(ssq[:, 4 + h:5 + h], ssq[:, 4 + h:5 + h]),
                         reads=[("ssq", 4 + h)], writes=[("ssq", 4 + h)])
                    S.op("vector", lambda e, pso=pso, h=h: e.scalar_tensor_tensor(tmp[:, 2 + h % 2, :], pso[:, :], ssq[:, 4 + h:5 + h], gng[:], ALU.mult, ALU.mult),
                         reads=[pko, ("ssq", 4 + h), "gng"], writes=[("tmp", 2 + h % 2)])
                    S.op("vector", lambda e, h=h, tt=tt: e.tensor_tensor(tmpb[:, h % 2, :], tmp[:, 2 + h % 2, :], sgt[:, tt, h * DV:(h + 1) * DV], ALU.mult),
                         reads=[("tmp", 2 + h % 2), ("sg", tt)], writes=[("tmpb", h % 2)])
                    def f(e, h=h):
                        ins = None
                        for vt in range(VT):
                            ins = e.transpose(ptb[:, vt * P:(vt + 1) * P], tmpb[:, h % 2, vt * P:(vt + 1) * P], identb[:])
                        return ins
                    S.op("tensor", f, reads=[("tmpb", h % 2), "identb"], writes=[ptk])
                    for vt in range(VT):
                        S.op("scalar", lambda e, h=h, vt=vt, tsl=tsl: e.activation(
                            hnT[:, CT + h * VT + vt, tsl], ptb[:, vt * P:(vt + 1) * P], AF.Identity, bias=c_zero, scale=1.0),
                            reads=[ptk, "col"], writes=[("hn", CT + h * VT + vt)])
                for h in range(H):
                    S.op("scalar", lambda e, h=h: e.activation(Sb[:, h * DKT:(h + 1) * DKT, :], St[:, h * DKT:(h + 1) * DKT, :], AF.Identity, bias=c_zero, scale=1.0),
                         reads=[("St", h, dt) for dt in range(DKT)] + ["col"], writes=[("Sb", h)])

            def ev_o(j, ps, pk):
                S.op("vector", lambda e: e.scalar_tensor_tensor(x_sb[:, j, :], ps[:, 0:TS], mcol(2, j), x_sb[:, j, :], ALU.mult, ALU.add),
                     reads=[pk, "modl", ("x", j)], writes=[("x", j)])
            gemm_fm(w_out[l], 0, KT, hnT, [("hn", kt) for kt in range(MT)], MT, ev_o)

            norm_to_hnT(A2, 3)
            for fb in range(NFB):
                def ev_h(j, ps, pk, fb=fb):
                    S.op("scalar", lambda e: e.activation(tmp[:, j % 2, 0:TS], ps[:, 0:TS], AF.Relu, bias=c_zero, scale=1.0),
                         reads=[pk, "col"], writes=[("tmp", j % 2)])
                    S.op("vector", lambda e: e.tensor_tensor(h1T[:, j, :], tmp[:, j % 2, 0:TS], tmp[:, j % 2, 0:TS], ALU.mult),
                         reads=[("tmp", j % 2)], writes=[("h1", j)])
                gemm_fm(w_m1[l], fb * FB, FBT, hnT, HN_ALL, KT, ev_h)

                def ev_f(j, ps, pk):
                    S.op("vector", lambda e: e.scalar_tensor_tensor(x_sb[:, j, :], ps[:, 0:TS], mcol(5, j), x_sb[:, j, :], ALU.mult, ALU.add),
                         reads=[pk, "modl", ("x", j)], writes=[("x", j)])
                gemm_fm(w_m2[l][fb * FB:(fb + 1) * FB, :], 0, KT, h1T, [("h1", j) for j in range(FBT)], FBT, ev_f)

            if not last:
                S.op("sync", lambda e, t0=t0: e.dma_start(
                    out=xs[:, t0:t0 + TS].rearrange("(kt p) t -> p kt t", p=P), in_=x_sb[:]),
                    reads=[("x", kt) for kt in range(KT)], writes=["xdram"], dma_sem="dxo")
            else:
                ps, pk = nextps()
                for kt in (range(KT) if FINAL else ()):
                    S.op("scalar", lambda e, kt=kt: e.activation(tmp[:, kt % 2, 0:TS], x_sb[:, kt, :], AF.Square, bias=c_zero, scale=1.0),
                         reads=[("x", kt), "col"], writes=[("tmp", kt % 2)])
                    S.op("tensor", lambda e, kt=kt, ps=ps: e.matmul(ps[:, 0:TS], ones_f, tmp[:, kt % 2, 0:TS], start=(kt == 0), stop=(kt == KT - 1)),
                         reads=[("tmp", kt % 2), "cs"], writes=[pk])
                if FINAL:
                    S.op("scalar", lambda e, ps=ps: e.activation(rs[:, 0, :], ps[:, 0:TS], AF.Sqrt, bias=c_eps, scale=1.0 / D), reads=[pk, "col"], writes=[("rs", 0)])
                    S.op("vector", lambda e: e.reciprocal(rs[:, 0, :], rs[:, 0, :]), reads=[("rs", 0)], writes=[("rs", 0)])
                for kt in (range(KT) if FINAL else ()):
                    S.op("vector", lambda e, kt=kt: e.scalar_tensor_tensor(x_sb[:, kt, :], x_sb[:, kt, :], fgt[:, kt:kt + 1], rs[:, 0, :], ALU.mult, ALU.mult),
                         reads=[("x", kt), ("rs", 0), "fgt"], writes=[("x", kt)])
                S.op("sync", lambda e, t0=t0: e.dma_start(
                    out=outT[:, t0:t0 + TS].rearrange("(kt p) t -> p kt t", p=P), in_=x_sb[:]),
                    reads=[("x", kt) for kt in range(KT)], writes=["odram"], dma_sem="dxo")

    S.final_wait("sync", ["dxo"])

    semnames = sorted(S.cnt.keys())
    sem_ctx = [nc.semaphore(n) for n in semnames]
    sems = {n: c.__enter__() for n, c in zip(semnames, sem_ctx)}
    bounds = [{e: 0 for e in S.ENGS}] + S.marks + [{e: len(S.streams[e]) for e in S.ENGS}]
    for lo, hi in zip(bounds[:-1], bounds[1:]):
        with nc.Block() as block:
            def replay(engname, lo=lo, hi=hi):
                def run(eng):
                    for wl, fn, sem, inc in S.streams[engname][lo[engname]:hi[engname]]:
                        for s, v in wl:
                            eng.wait_ge(sems[s], v)
                        if fn is not None:
                            fn(eng).then_inc(sems[sem], inc)
                return run
            block.tensor(replay("tensor"))
            block.vector(replay("vector"))
            block.scalar(replay("scalar"))
            block.gpsimd(replay("gpsimd"))
            block.sync(replay("sync"))
    for c in reversed(sem_ctx):
        c.__exit__(None, None, None)
    for t in reversed(ctx):
        t.__exit__(None, None, None)
    return nc


def make_consts():
    m = np.arange(P)[:, None]
    l = np.arange(P)[None, :]
    ones = np.ones((P, P), np.float32)
    tri = np.where(m <= l, -1.0 / TAU, 0.0).astype(np.float32)
    sutri = np.where(m > l, -1.0 / TAU, 0.0).astype(np.float32)
    ident = np.eye(P, dtype=np.float32)
    masku = np.where(m <= l, 1.0, 0.0).astype(np.float32)
    return np.ascontiguousarray(np.concatenate([ones, tri, sutri, ident, masku], axis=1))


def colT(v, nt):
    return np.ascontiguousarray(np.asarray(v, np.float32).reshape(nt, P).T)


def prep_shared(cfg, inp):
    DEPTH, D, CC, H, DK, DV = (cfg[k] for k in ("DEPTH", "D", "CC", "H", "DK", "DV"))
    KT, CT, QW = D // P, CC // P, H * DK
    sh = {}
    sh["w_ada"] = np.ascontiguousarray(inp["w_ada"], dtype=np.float32)
    sh["b_adaT"] = colT(inp["b_ada"], 6 * KT)
    sh["modtabT"] = np.stack([colT(inp["mod_table"][l].reshape(-1), 6 * KT) for l in range(DEPTH)])
    sh["n1gT"] = np.stack([colT(inp["norm1_g"][l], KT) for l in range(DEPTH)])
    sh["n2gT"] = np.stack([colT(inp["norm2_g"][l], KT) for l in range(DEPTH)])
    sh["fgT"] = colT(inp["final_g"], KT)
    sh["w_in"] = np.ascontiguousarray(inp["w_in"], dtype=np.float32)
    sh["w_out"] = np.ascontiguousarray(inp["w_out"], dtype=np.float32)
    sh["w_m1"] = np.ascontiguousarray(inp["w_mlp1"], dtype=np.float32)
    sh["w_m2"] = np.ascontiguousarray(inp["w_mlp2"], dtype=np.float32)
    cwl = []
    for l in range(DEPTH):
        cwt = np.asarray(inp["conv_w"][l], np.float32).T.reshape(CT, P, CONV_W)
        cwl.append(np.ascontiguousarray(cwt.transpose(1, 0, 2).reshape(P, CT * CONV_W)))
    sh["convwT"] = np.stack(cwl)
    sh["convbT"] = np.stack([colT(inp["conv_b"][l], CT) for l in range(DEPTH)])
    sh["lngT"] = np.stack([colT(inp["conv_ln_g"][l], CT) for l in range(DEPTH)])
    sh["lnbT"] = np.stack([colT(inp["conv_ln_b"][l], CT) for l in range(DEPTH)])
    wa = np.zeros((DEPTH, 32, QW), np.float32)
    wa[:, 0:cfg["LR"], :] = inp["w_alpha"]
    wa[:, cfg["LR"], :] = inp["b_alpha"]
    sh["walpha"] = wa
    sh["gngbc"] = np.ascontiguousarray(np.broadcast_to(np.asarray(inp["gla_norm_g"], np.float32)[:, None, :], (DEPTH, P, DV)))
    sh["cst"] = make_consts()
    return sh


def run_cfg(cfg, inp, trace=False):
    nc = build_program(cfg)
    sh = prep_shared(cfg, inp)
    KT = cfg["D"] // P
    ncores = cfg["NCORES"]
    in_maps = []
    for b in range(ncores):
        m = dict(sh)
        m["xT"] = np.ascontiguousarray(np.asarray(inp["x"][b], np.float32).T)
        m["scT"] = colT(inp["c"][b], KT)
        in_maps.append(m)
    res = run_bass_kernel_spmd(nc, in_maps, core_ids=list(range(ncores)), trace=trace)
    out = np.stack([np.ascontiguousarray(res.results[b]["outT"].T) for b in range(ncores)])
    return out, res


PER_LAYER = ("mod_table", "norm1_g", "w_in", "conv_w", "conv_b", "conv_ln_g", "conv_ln_b", "w_alpha",
             "b_alpha", "gla_norm_g", "w_out", "norm2_g", "w_mlp1", "w_mlp2")
_PROGS = {}


def run_layers(cfg, inp, xT_list, l0):
    key = (cfg["DEPTH"], cfg["NT"], cfg.get("FINAL", True), cfg["NCORES"])
    if key not in _PROGS:
        _PROGS[key] = build_program(cfg)
    nc = _PROGS[key]
    sub = {k: (np.asarray(v)[l0:l0 + cfg["DEPTH"]] if k in PER_LAYER else v) for k, v in inp.items()}
    sh = prep_shared(cfg, sub)
    KT = cfg["D"] // P
    in_maps = []
    for b in range(cfg["NCORES"]):
        m = dict(sh)
        m["xT"] = xT_list[b]
        m["scT"] = colT(inp["c"][b], KT)
        in_maps.append(m)
    res = run_bass_kernel_spmd(nc, in_maps, core_ids=list(range(cfg["NCORES"])))
    return [np.ascontiguousarray(res.results[b]["outT"]) for b in range(cfg["NCORES"])]


LAYERS_PER_LAUNCH = 1


def kernel(**inputs):
    B = FULL_CFG["NCORES"]
    cur = [np.ascontiguousarray(np.asarray(inputs["x"][b], np.float32).T) for b in range(B)]
    depth = FULL_CFG["DEPTH"]
    for l0 in range(0, depth, LAYERS_PER_LAUNCH):
        cfg = dict(FULL_CFG, DEPTH=LAYERS_PER_LAUNCH, FINAL=(l0 + LAYERS_PER_LAUNCH >= depth))
        cur = run_layers(cfg, inputs, cur, l0)
    return np.stack([np.ascontiguousarray(c.T) for c in cur]).astype(np.float32)
```

```python
import numpy as np
import concourse.bass as bass
import concourse.mybir as mybir
from concourse.bass_utils import run_bass_kernel_spmd

F32 = mybir.dt.float32
BF16 = mybir.dt.bfloat16
ALU = mybir.AluOpType
AF = mybir.ActivationFunctionType
P = 128
CONV_W = 31
HALO = CONV_W - 1
EPS = 1e-6
TAU = 16.0

FULL_CFG = dict(NT=4096, DEPTH=4, D=4096, CC=2048, H=4, DK=256, DV=512, LR=16, DFF=16384,
                TS=256, FB=1024, NCORES=4)


class Sched:
    ENGS = ("tensor", "vector", "scalar", "gpsimd", "sync")

    def __init__(self):
        self.streams = {e: [] for e in self.ENGS}
        self.cnt = {}
        self.waited = {e: {} for e in self.ENGS}
        self.lastw = {}
        self.readers = {}
        self.epoch = {}
        self.last_dma = {}
        self.marks = []

    LIMIT = 30000
    CSCALE = 1
    DSCALE = 1

    def op(self, eng, fn, reads=(), writes=(), dma_sem=None):
        needs = {}

        def need(s, v):
            if needs.get(s, 0) < v:
                needs[s] = v
        for b in reads:
            lw = self.lastw.get(b)
            if lw:
                need(*lw)
        for b in writes:
            lw = self.lastw.get(b)
            if lw:
                need(*lw)
            for s, v in self.readers.get(b, {}).items():
                need(s, v)
        if dma_sem is None:
            base, inc = "c_" + eng, self.CSCALE
        else:
            base, inc = dma_sem, 16 * self.DSCALE
            ld = self.last_dma.get(base)
            if ld:
                need(*ld)
        ep = self.epoch.get(base, 0)
        sem = "%s_e%d" % (base, ep)
        if self.cnt.get(sem, 0) + inc > self.LIMIT:
            ep += 1
            self.epoch[base] = ep
            sem = "%s_e%d" % (base, ep)
        wl = []
        wd = self.waited[eng]
        for s, v in needs.items():
            if wd.get(s, 0) < v:
                wd[s] = v
                wl.append((s, v))
        val = self.cnt.get(sem, 0) + inc
        self.cnt[sem] = val
        if dma_sem is not None:
            self.last_dma[base] = (sem, val)
        self.streams[eng].append((wl, fn, sem, inc))
        for b in reads:
            r = self.readers.setdefault(b, {})
            if r.get(sem, 0) < val:
                r[sem] = val
        for b in writes:
            self.lastw[b] = (sem, val)
            self.readers[b] = {}

    def mark(self):
        self.marks.append({e: len(self.streams[e]) for e in self.ENGS})

    def final_wait(self, eng, sems):
        wl = [self.last_dma[s] for s in sems if s in self.last_dma]
        self.streams[eng].append((wl, None, None, 0))


def build_program(cfg):
    NT, DEPTH, D, CC, H, DK, DV, LR, DFF, TS, FB = (cfg[k] for k in
        ("NT", "DEPTH", "D", "CC", "H", "DK", "DV", "LR", "DFF", "TS", "FB"))
    KT = D // P
    CT = CC // P
    QW = H * DK
    QT = QW // P
    DKT = DK // P
    VW = H * DV
    VT = DV // P
    MIX = CC + VW
    MT = MIX // P
    IN_COLS = 2 * CC + 2 * QW + 2 * VW + LR
    O_AV, O_AG, O_Q, O_K, O_V, O_G, O_LR = 0, CC, 2 * CC, 2 * CC + QW, 2 * CC + 2 * QW, 2 * CC + 2 * QW + VW, 2 * CC + 2 * QW + 2 * VW
    NST = NT // TS
    FINAL = cfg.get("FINAL", True)
    NTT = TS // P
    NFB = DFF // FB
    FBT = FB // P
    WSLOT = 4096
    NSLOT = 3
    assert DV == 512 and TS % P == 0 and TS <= 512 and KT * P <= WSLOT

    nc = bass.Bass("TRN2", target_bir_lowering=False)

    def din(name, shape, dt=F32):
        return nc.dram_tensor(name, list(shape), dt, kind="ExternalInput")
    xT_in = din("xT", [D, NT])
    scT_in = din("scT", [P, KT])
    wadab = din("wadab", [6 * KT, P, KT * P])
    b_adaT = din("b_adaT", [P, 6 * KT])
    modtabT = din("modtabT", [DEPTH, P, 6 * KT])
    n1gT = din("n1gT", [DEPTH, P, KT])
    n2gT = din("n2gT", [DEPTH, P, KT])
    fgT = din("fgT", [P, KT])
    winb = din("winb", [DEPTH, (IN_COLS - LR) // P, P, KT * P])
    wlr = din("wlr", [DEPTH, P, KT * LR])
    woutb = din("woutb", [DEPTH, D // P, P, MT * P])
    wm1b = din("wm1b", [DEPTH, DFF // P, P, KT * P])
    wm2b = din("wm2b", [DEPTH, DFF // FB, D // 512, P, (FB // P) * 512])
    convwT = din("convwT", [DEPTH, P, CT * CONV_W])
    convbT = din("convbT", [DEPTH, P, CT])
    lngT = din("lngT", [DEPTH, P, CT])
    lnbT = din("lnbT", [DEPTH, P, CT])
    walpha = din("walpha", [DEPTH, 32, QW])
    gngbc = din("gngbc", [DEPTH, P, DV])
    cst = din("cst", [P, 5 * P])
    outT = nc.dram_tensor("outT", [D, NT], F32, kind="ExternalOutput")
    xs = nc.dram_tensor("xs", [D, NT], F32)
    winh = nc.dram_tensor("winh", [(IN_COLS - LR) // P, P, KT * P], BF16)
    wlrh = nc.dram_tensor("wlrh", [P, KT * LR], BF16)
    wouth = nc.dram_tensor("wouth", [D // P, P, MT * P], BF16)
    wm1h = nc.dram_tensor("wm1h", [DFF // P, P, KT * P], BF16)
    wm2h = nc.dram_tensor("wm2h", [DFF // FB, D // 512, P, (FB // P) * 512], BF16)

    S = Sched()
    tiles = {}
    ctx = []

    def sb(name, shape, dt=F32):
        t = nc.sbuf_tensor(name, list(shape), dt)
        ctx.append(t)
        tiles[name] = t.__enter__()
        return tiles[name]

    x_sb = sb("x_sb", [P, KT, TS])
    hnT = sb("hnT", [P, KT, TS], BF16)
    wsl = [sb("wsl%d" % i, [P, WSLOT], BF16) for i in range(NSLOT)]
    stg = sb("stg", [P, WSLOT])
    u_ext = sb("u_ext", [P, CT, HALO + TS], BF16)
    assert CT == 2 * QT
    EE = sb("EE", [P, 2 * QT, TS])
    Ep = EE[:, 0:QT, :]
    En = EE[:, QT:2 * QT, :]
    yc = EE
    dec = sb("dec", [P, QT, NTT])
    lsp = sb("lsp", [P, NTT, QW])
    qdT = sb("qdT", [P, QT, TS], BF16)
    kiT = sb("kiT", [P, QT, TS], BF16)
    kst = sb("kst", [P, NTT, QW], BF16)
    vtk = sb("vtk", [P, NTT, VW], BF16)
    sgt = sb("sgt", [P, NTT, VW], BF16)
    St = sb("St", [P, H * DKT, DV])
    Sb = sb("Sb", [P, H * DKT, DV], BF16)
    h1T = sb("h1T", [P, FBT, TS], BF16)
    rs = sb("rs", [P, 4, TS])
    tmp = sb("tmp", [P, 4, 512])
    tmpb = sb("tmpb", [P, 2, 512], BF16)
    scm = sb("scm", [P, 2, P], BF16)
    modc = sb("modc", [P, 6 * KT])
    modl = sb("modl", [P, 6 * KT])
    par = sb("par", [P, 4 * KT + 4 * CT + CT * CONV_W])
    gng = sb("gng", [P, DV])
    wal = sb("wal", [32, QW])
    cs = sb("cs", [P, 5 * P])
    identb = sb("identb", [P, P], BF16)
    masku = sb("masku", [P, P], BF16)
    alr = sb("alr", [32, TS])
    sct = sb("sct", [P, KT])
    sctb = sb("sctb", [P, KT], BF16)
    col = sb("col", [P, 8])
    ssq = sb("ssq", [P, 8])

    fgt = sb("fgt", [P, KT])
    pst = []
    NPS = 7
    for i in range(NPS):
        t = nc.psum_tensor("ps%d" % i, [P, 512], F32)
        ctx.append(t)
        pst.append(t.__enter__())
    t = nc.psum_tensor("ptb", [P, 1024], BF16)
    ctx.append(t)
    ptb = t.__enter__()
    ptk = "ptb"
    ps_state = {"i": 0}

    def nextps():
        i = ps_state["i"]
        ps_state["i"] = (i + 1) % NPS
        return pst[i], ("ps", i)

    ws_state = {"i": 0}

    def wload(src3, a, b):
        i = ws_state["i"]
        ws_state["i"] = (i + 1) % NSLOT
        dst = wsl[i][:, 0:a * b].rearrange("p (a b) -> p a b", b=b)
        sv = stg[:, 0:a * b].rearrange("p (a b) -> p a b", b=b)
        S.op("sync", lambda e, src3=src3, n=a * b: e.dma_start(out=stg[:, 0:n], in_=src3),
             writes=["stg"], dma_sem="dstg")
        S.op("gpsimd", lambda e, dst=dst, sv=sv: e.tensor_copy(dst, sv), reads=["stg"], writes=[("w", i)])
        return dst, ("w", i)

    def wload_h(src2d, a, b):
        i = ws_state["i"]
        ws_state["i"] = (i + 1) % NSLOT
        dst = wsl[i][:, 0:a * b].rearrange("p (a b) -> p a b", b=b)
        S.op("sync", lambda e, src2d=src2d, n=a * b, i=i: e.dma_start(out=wsl[i][:, 0:n], in_=src2d),
             reads=["wbf"], writes=[("w", i)], dma_sem="dw%d" % i)
        return dst, ("w", i)

    def convert(src2d, dst2d, n):
        wv, wk = wload(src2d, 1, n)
        i = wk[1]
        S.op("sync", lambda e, dst2d=dst2d, n=n, i=i: e.dma_start(out=dst2d, in_=wsl[i][:, 0:n]),
             reads=[wk], writes=["wbf"], dma_sem="dcv")

    ones_f = cs[:, 0:P]
    triN = cs[:, P:2 * P]
    sutriN = cs[:, 2 * P:3 * P]
    c_one = col[:, 0:1]
    c_zero = col[:, 1:2]
    c_eps = col[:, 2:3]

    S.op("sync", lambda e: e.dma_start(out=cs[:], in_=cst[:, :]), writes=["cs"], dma_sem="dpar")
    S.op("sync", lambda e: e.dma_start(out=sct[:], in_=scT_in[:, :]), writes=["sct"], dma_sem="dpar")
    S.op("sync", lambda e: e.dma_start(out=modc[:], in_=b_adaT[:, :]), writes=["modc"], dma_sem="dpar")
    S.op("sync", lambda e: e.dma_start(out=fgt[:], in_=fgT[:, :]), writes=["fgt"], dma_sem="dpar")
    S.op("vector", lambda e: e.tensor_copy(identb[:], cs[:, 3 * P:4 * P]), reads=["cs"], writes=["identb"])
    S.op("vector", lambda e: e.tensor_copy(masku[:], cs[:, 4 * P:5 * P]), reads=["cs"], writes=["masku"])
    S.op("vector", lambda e: e.memset(col[:, 0:1], 1.0), writes=["col"])
    S.op("vector", lambda e: e.memset(col[:, 1:2], 0.0), writes=["col"])
    S.op("vector", lambda e: e.memset(col[:, 2:3], EPS), writes=["col"])
    S.op("vector", lambda e: e.memset(alr[:], 1.0), writes=["alr"])
    S.op("scalar", lambda e: e.activation(sctb[:], sct[:], AF.Silu, bias=c_zero, scale=1.0),
         reads=["sct", "col"], writes=["sctb"])
    NJ = 6 * KT
    ps_mod, k_mod = nextps()
    for j0 in range(0, NJ, 1):
        wv, wk = wload(wadab[j0, :, :], KT, P)

        def f(e, wv=wv, j0=j0):
            ins = None
            for jj in range(1):
                for kt in range(KT):
                    ins = e.matmul(ps_mod[:, j0 + jj:j0 + jj + 1], wv[:, kt, jj * P:(jj + 1) * P],
                                   sctb[:, kt:kt + 1], start=(kt == 0), stop=(kt == KT - 1))
            return ins
        S.op("tensor", f, reads=[wk, "sctb"], writes=[k_mod])
    S.op("vector", lambda e: e.tensor_tensor(modc[:], modc[:], ps_mod[:, 0:NJ], ALU.add),
         reads=[k_mod, "modc"], writes=["modc"])

    A1 = par[:, 0:KT]
    A2 = par[:, KT:2 * KT]
    n1g = par[:, 2 * KT:3 * KT]
    n2g = par[:, 3 * KT:4 * KT]
    o0 = 4 * KT
    convb = par[:, o0:o0 + CT]
    lng = par[:, o0 + CT:o0 + 2 * CT]
    lnb = par[:, o0 + 2 * CT:o0 + 3 * CT]
    cw = par[:, o0 + 4 * CT:o0 + 4 * CT + CT * CONV_W]

    def mcol(six, kt):
        return modl[:, six * KT + kt:six * KT + kt + 1]

    def norm_to_hnT(Acols, six_shift):
        ps, pk = nextps()
        for kt in range(KT):
            S.op("scalar", lambda e, kt=kt: e.activation(tmp[:, kt % 2, 0:TS], x_sb[:, kt, :], AF.Square,
                                                         bias=c_zero, scale=1.0),
                 reads=[("x", kt), "col"], writes=[("tmp", kt % 2)])
            S.op("tensor", lambda e, kt=kt: e.matmul(ps[:, 0:TS], ones_f, tmp[:, kt % 2, 0:TS],
                                                     start=(kt == 0), stop=(kt == KT - 1)),
                 reads=[("tmp", kt % 2), "cs"], writes=[pk])
        S.op("scalar", lambda e: e.activation(rs[:, 0, :], ps[:, 0:TS], AF.Sqrt, bias=c_eps, scale=1.0 / D),
             reads=[pk, "col"], writes=[("rs", 0)])
        S.op("vector", lambda e: e.reciprocal(rs[:, 0, :], rs[:, 0, :]),
             reads=[("rs", 0)], writes=[("rs", 0)])
        for kt in range(KT):
            S.op("vector", lambda e, kt=kt: e.scalar_tensor_tensor(tmp[:, 2 + kt % 2, 0:TS], x_sb[:, kt, :],
                                                                  Acols[:, kt:kt + 1], rs[:, 0, :], ALU.mult, ALU.mult),
                 reads=[("x", kt), ("rs", 0), "par"], writes=[("tmp", 2 + kt % 2)])
            S.op("scalar", lambda e, kt=kt: e.activation(hnT[:, kt, :], tmp[:, 2 + kt % 2, 0:TS], AF.Identity,
                                                         bias=mcol(six_shift, kt), scale=1.0),
                 reads=[("tmp", 2 + kt % 2), "modl"], writes=[("hn", kt)])

    HN_ALL = [("hn", kt) for kt in range(KT)]

    def gemm_fm(blocks, rhs3, rhs_keys, nk, evac):
        j = 0
        for src2d, nb in blocks:
            wv, wk = wload_h(src2d, nk, nb * P)
            for jj in range(nb):
                ps, pk = nextps()

                def f(e, wv=wv, jj=jj, ps=ps):
                    ins = None
                    for kt in range(nk):
                        ins = e.matmul(ps[:, 0:TS], wv[:, kt, jj * P:(jj + 1) * P], rhs3[:, kt, :],
                                       start=(kt == 0), stop=(kt == nk - 1))
                    return ins
                S.op("tensor", f, reads=[wk] + rhs_keys, writes=[pk])
                evac(j + jj, ps, pk)
            j += nb

    def gemm_tm(wb_l, c0, ncols, evac):
        for cb in range(0, ncols, P):
            wv, wk = wload_h(wb_l[(c0 + cb) // P, :, :], KT, P)
            for tt in range(NTT):
                ps, pk = nextps()

                def f(e, wv=wv, tt=tt, ps=ps):
                    ins = None
                    for kt in range(KT):
                        ins = e.matmul(ps[:, 0:P], hnT[:, kt, tt * P:(tt + 1) * P], wv[:, kt, :],
                                       start=(kt == 0), stop=(kt == KT - 1))
                    return ins
                S.op("tensor", f, reads=[wk] + HN_ALL, writes=[pk])
                evac(tt, cb, ps, pk)

    for l in range(DEPTH):
        last = (l == DEPTH - 1)
        S.op("sync", lambda e, l=l: e.dma_start(out=modl[:], in_=modtabT[l, :, :]), writes=["modl"], dma_sem="dpar")
        S.op("sync", lambda e, l=l: e.dma_start(out=par[:, 2 * KT:3 * KT], in_=n1gT[l, :, :]), writes=["par"], dma_sem="dpar")
        S.op("sync", lambda e, l=l: e.dma_start(out=par[:, 3 * KT:4 * KT], in_=n2gT[l, :, :]), writes=["par"], dma_sem="dpar")
        S.op("sync", lambda e, l=l: e.dma_start(out=par[:, o0:o0 + CT], in_=convbT[l, :, :]), writes=["par"], dma_sem="dpar")
        S.op("sync", lambda e, l=l: e.dma_start(out=par[:, o0 + CT:o0 + 2 * CT], in_=lngT[l, :, :]), writes=["par"], dma_sem="dpar")
        S.op("sync", lambda e, l=l: e.dma_start(out=par[:, o0 + 2 * CT:o0 + 3 * CT], in_=lnbT[l, :, :]), writes=["par"], dma_sem="dpar")
        S.op("sync", lambda e, l=l: e.dma_start(out=par[:, o0 + 4 * CT:o0 + 4 * CT + CT * CONV_W], in_=convwT[l, :, :]), writes=["par"], dma_sem="dpar")
        S.op("sync", lambda e, l=l: e.dma_start(out=gng[:], in_=gngbc[l, :, :]), writes=["gng"], dma_sem="dpar")
        S.op("sync", lambda e, l=l: e.dma_start(out=wal[:], in_=walpha[l, :, :]), writes=["wal"], dma_sem="dpar")
        S.op("vector", lambda e: e.tensor_tensor(modl[:], modl[:], modc[:], ALU.add), reads=["modl", "modc"], writes=["modl"])
        S.op("vector", lambda e: e.scalar_tensor_tensor(par[:, 0:KT], modl[:, KT:2 * KT], 1.0, par[:, 2 * KT:3 * KT], ALU.add, ALU.mult),
             reads=["modl", "par"], writes=["par"])
        S.op("vector", lambda e: e.scalar_tensor_tensor(par[:, KT:2 * KT], modl[:, 4 * KT:5 * KT], 1.0, par[:, 3 * KT:4 * KT], ALU.add, ALU.mult),
             reads=["modl", "par"], writes=["par"])
        S.op("vector", lambda e: e.memset(St[:], 0.0), writes=["St"])
        S.op("vector", lambda e: e.memset(Sb[:], 0.0), writes=["Sb"])
        S.op("vector", lambda e: e.memset(u_ext[:, :, 0:HALO], 0.0), writes=[("u", j) for j in range(CT)])

        S.mark()
        for j in range((IN_COLS - LR) // P):
            convert(winb[l, j, :, :], winh[j, :, :], KT * P)
        convert(wlr[l, :, :], wlrh[:, :], KT * LR)
        for j in range(D // P):
            convert(woutb[l, j, :, :], wouth[j, :, :], MT * P)
        for j in range(DFF // P):
            convert(wm1b[l, j, :, :], wm1h[j, :, :], KT * P)
        for fb in range(DFF // FB):
            for jb in range(D // 512):
                convert(wm2b[l, fb, jb, :, :], wm2h[fb, jb, :, :], (FB // P) * 512)
        xsrc = xT_in if l == 0 else xs
        for st in range(NST):
            t0 = st * TS
            S.mark()
            S.op("sync", lambda e, xsrc=xsrc, t0=t0: e.dma_start(
                out=x_sb[:], in_=xsrc[:, t0:t0 + TS].rearrange("(kt p) t -> p kt t", p=P)),
                reads=["xdram"], writes=[("x", kt) for kt in range(KT)], dma_sem="dx")
            norm_to_hnT(A1, 0)
            win = winh
            wv, wk = wload_h(wlrh[:, :], KT, LR)
            ps, pk = nextps()

            def f(e, wv=wv, ps=ps):
                ins = None
                for kt in range(KT):
                    ins = e.matmul(ps[0:LR, 0:TS], wv[:, kt, :], hnT[:, kt, :], start=(kt == 0), stop=(kt == KT - 1))
                return ins
            S.op("tensor", f, reads=[wk] + HN_ALL, writes=[pk])
            S.op("vector", lambda e, ps=ps: e.tensor_copy(alr[0:LR, :], ps[0:LR, 0:TS]), reads=[pk], writes=["alr"])
            for tt in range(NTT):
                for hf in range(QW // 512):
                    ps, pk = nextps()
                    S.op("tensor", lambda e, ps=ps, tt=tt, hf=hf: e.matmul(
                        ps[:, :], alr[0:LR + 1, tt * P:(tt + 1) * P], wal[0:LR + 1, hf * 512:(hf + 1) * 512], start=True, stop=True),
                        reads=["alr", "wal"], writes=[pk])
                    S.op("scalar", lambda e, ps=ps, tt=tt, hf=hf: e.activation(
                        lsp[:, tt, hf * 512:(hf + 1) * 512], ps[:, :], AF.Exp, bias=c_zero, scale=-1.0),
                        reads=[pk, "col"], writes=[("lsp", tt, hf)])
                    S.op("scalar", lambda e, tt=tt, hf=hf: e.activation(
                        lsp[:, tt, hf * 512:(hf + 1) * 512], lsp[:, tt, hf * 512:(hf + 1) * 512], AF.Ln, bias=c_one, scale=1.0),
                        reads=[("lsp", tt, hf), "col"], writes=[("lsp", tt, hf)])
            for dt in range(QT):
                ps, pk = nextps()

                def f(e, ps=ps, dt=dt):
                    ins = None
                    for tt in range(NTT):
                        ins = e.matmul(ps[:, tt * P:(tt + 1) * P], lsp[:, tt, dt * P:(dt + 1) * P], triN, start=True, stop=True)
                    return ins
                S.op("tensor", f, reads=[("lsp", tt, (dt * P) // 512) for tt in range(NTT)] + ["cs"], writes=[pk])
                S.op("scalar", lambda e, ps=ps, dt=dt: e.activation(Ep[:, dt, :], ps[:, 0:TS], AF.Exp, bias=c_zero, scale=1.0),
                     reads=[pk, "col"], writes=[("EE", dt)])
                S.op("scalar", lambda e, ps=ps, dt=dt: e.activation(En[:, dt, :], ps[:, 0:TS], AF.Exp, bias=c_zero, scale=-1.0),
                     reads=[pk, "col"], writes=[("EE", QT + dt)])
                for tt in range(NTT):
                    S.op("vector", lambda e, dt=dt, tt=tt: e.tensor_copy(dec[:, dt, tt:tt + 1], Ep[:, dt, tt * P + P - 1:tt * P + P]),
                         reads=[("EE", dt)], writes=[("dec", dt)])
            for tt in range(NTT):
                for hf in range(QW // 512):
                    ps, pk = nextps()
                    S.op("tensor", lambda e, ps=ps, tt=tt, hf=hf: e.matmul(
                        ps[:, :], sutriN, lsp[:, tt, hf * 512:(hf + 1) * 512], start=True, stop=True),
                        reads=[("lsp", tt, hf), "cs"], writes=[pk])
                    S.op("scalar", lambda e, ps=ps, tt=tt, hf=hf: e.activation(
                        lsp[:, tt, hf * 512:(hf + 1) * 512], ps[:, :], AF.Exp, bias=c_zero, scale=1.0),
                        reads=[pk, "col"], writes=[("lsp", tt, hf)])

            def ev_q(j, ps, pk):
                S.op("vector", lambda e: e.scalar_tensor_tensor(qdT[:, j, :], ps[:, 0:TS], float(DK) ** -0.5, Ep[:, j, :], ALU.mult, ALU.mult),
                     reads=[pk, ("EE", j)], writes=[("qd", j)])
            gemm_fm([(win[O_Q // P + j, :, :], 1) for j in range(QT)], hnT, HN_ALL, KT, ev_q)

            def ev_k(j, ps, pk):
                S.op("vector", lambda e: e.tensor_tensor(kiT[:, j, :], ps[:, 0:TS], En[:, j, :], ALU.mult),
                     reads=[pk, ("EE", QT + j)], writes=[("ki", j)])
            gemm_fm([(win[O_K // P + j, :, :], 1) for j in range(QT)], hnT, HN_ALL, KT, ev_k)

            def ev_kt(tt, cb, ps, pk):
                S.op("vector", lambda e: e.tensor_tensor(kst[:, tt, cb:cb + P], ps[:, 0:P], lsp[:, tt, cb:cb + P], ALU.mult),
                     reads=[pk, ("lsp", tt, cb // 512)], writes=[("kst", tt)])
            gemm_tm(win, O_K, QW, ev_kt)

            def ev_v(tt, cb, ps, pk):
                S.op("scalar", lambda e: e.activation(vtk[:, tt, cb:cb + P], ps[:, 0:P], AF.Identity, bias=c_zero, scale=1.0),
                     reads=[pk, "col"], writes=[("v", tt)])
            gemm_tm(win, O_V, VW, ev_v)

            def ev_g(tt, cb, ps, pk):
                S.op("scalar", lambda e: e.activation(sgt[:, tt, cb:cb + P], ps[:, 0:P], AF.Silu, bias=c_zero, scale=1.0),
                     reads=[pk, "col"], writes=[("sg", tt)])
            gemm_tm(win, O_G, VW, ev_g)

            for j in range(CT):
                wv, wk = wload_h(win[O_AV // P + j, :, :], KT, P)
                wg, wgk = wload_h(win[O_AG // P + j, :, :], KT, P)
                psa, pka = nextps()
                psg, pkg = nextps()
                for (wv_, wk_, ps_, pk_) in ((wv, wk, psa, pka), (wg, wgk, psg, pkg)):
                    def f(e, wv_=wv_, ps_=ps_):
                        ins = None
                        for kt in range(KT):
                            ins = e.matmul(ps_[:, 0:TS], wv_[:, kt, :], hnT[:, kt, :], start=(kt == 0), stop=(kt == KT - 1))
                        return ins
                    S.op("tensor", f, reads=[wk_] + HN_ALL, writes=[pk_])
                S.op("scalar", lambda e, psg=psg, j=j: e.activation(tmp[:, j % 2, 0:TS], psg[:, 0:TS], AF.Sigmoid, bias=c_zero, scale=1.0),
                     reads=[pkg, "col"], writes=[("tmp", j % 2)])
                S.op("vector", lambda e, psa=psa, j=j: e.tensor_tensor(u_ext[:, j, HALO:HALO + TS], psa[:, 0:TS], tmp[:, j % 2, 0:TS], ALU.mult),
                     reads=[pka, ("tmp", j % 2)], writes=[("u", j)])

            for j in range(CT):
                S.op("vector", lambda e, j=j: e.tensor_scalar(yc[:, j, :], u_ext[:, j, 0:TS], cw[:, j * CONV_W:j * CONV_W + 1],
                                                              convb[:, j:j + 1], ALU.mult, ALU.add),
                     reads=[("u", j), "par"], writes=[("EE", j)])

                for tp in range(1, CONV_W):
                    S.op("vector", lambda e, j=j, tp=tp: e.scalar_tensor_tensor(
                        yc[:, j, :], u_ext[:, j, tp:tp + TS], cw[:, j * CONV_W + tp:j * CONV_W + tp + 1], yc[:, j, :], ALU.mult, ALU.add),
                        reads=[("u", j), "par", ("EE", j)], writes=[("EE", j)])
                S.op("scalar", lambda e, j=j: e.activation(u_ext[:, j, 0:HALO], u_ext[:, j, TS:TS + HALO], AF.Identity, bias=c_zero, scale=1.0),
                     reads=[("u", j), "col"], writes=[("u", j)])
            ps1, pk1 = nextps()
            ps2, pk2 = nextps()
            for j in range(CT):
                S.op("tensor", lambda e, j=j: e.matmul(ps1[:, 0:TS], ones_f, yc[:, j, :], start=(j == 0), stop=(j == CT - 1)),
                     reads=[("EE", j), "cs"], writes=[pk1])
                S.op("scalar", lambda e, j=j: e.activation(tmp[:, j % 2, 0:TS], yc[:, j, :], AF.Square, bias=c_zero, scale=1.0),
                     reads=[("EE", j), "col"], writes=[("tmp", j % 2)])
                S.op("tensor", lambda e, j=j: e.matmul(ps2[:, 0:TS], ones_f, tmp[:, j % 2, 0:TS], start=(j == 0), stop=(j == CT - 1)),
                     reads=[("tmp", j % 2), "cs"], writes=[pk2])
            S.op("vector", lambda e: e.tensor_scalar(rs[:, 1, :], ps1[:, 0:TS], 1.0 / CC, 0.0, ALU.mult, ALU.add), reads=[pk1], writes=[("rs", 1)])
            S.op("vector", lambda e: e.tensor_tensor(rs[:, 3, :], rs[:, 1, :], rs[:, 1, :], ALU.mult), reads=[("rs", 1)], writes=[("rs", 3)])
            S.op("vector", lambda e: e.scalar_tensor_tensor(rs[:, 2, :], ps2[:, 0:TS], 1.0 / CC, rs[:, 3, :], ALU.mult, ALU.subtract),
                 reads=[pk2, ("rs", 3)], writes=[("rs", 2)])
            S.op("scalar", lambda e: e.activation(rs[:, 2, :], rs[:, 2, :], AF.Sqrt, bias=c_eps, scale=1.0), reads=[("rs", 2), "col"], writes=[("rs", 2)])
            S.op("vector", lambda e: e.reciprocal(rs[:, 2, :], rs[:, 2, :]), reads=[("rs", 2)], writes=[("rs", 2)])
            for j in range(CT):
                S.op("vector", lambda e, j=j: e.tensor_tensor(tmp[:, 2 + j % 2, 0:TS], yc[:, j, :], rs[:, 1, :], ALU.subtract),
                     reads=[("EE", j), ("rs", 1)], writes=[("tmp", 2 + j % 2)])
                S.op("vector", lambda e, j=j: e.tensor_tensor(tmp[:, 2 + j % 2, 0:TS], tmp[:, 2 + j % 2, 0:TS], rs[:, 2, :], ALU.mult),
                     reads=[("tmp", 2 + j % 2), ("rs", 2)], writes=[("tmp", 2 + j % 2)])
                S.op("scalar", lambda e, j=j: e.activation(hnT[:, j, :], tmp[:, 2 + j % 2, 0:TS], AF.Silu, bias=lnb[:, j:j + 1], scale=lng[:, j:j + 1]),
                     reads=[("tmp", 2 + j % 2), "par"], writes=[("hn", j)])

            for tt in range(NTT):
                tsl = slice(tt * P, (tt + 1) * P)
                for h in range(H):
                    ps, pk = nextps()

                    def f(e, ps=ps, h=h, tsl=tsl):
                        ins = None
                        for dt in range(DKT):
                            ins = e.matmul(ps[:, 0:P], kiT[:, h * DKT + dt, tsl], qdT[:, h * DKT + dt, tsl], start=(dt == 0), stop=(dt == DKT - 1))
                        return ins
                    S.op("tensor", f, reads=[("ki", h * DKT + dt) for dt in range(DKT)] + [("qd", h * DKT + dt) for dt in range(DKT)], writes=[pk])
                    S.op("vector", lambda e, ps=ps, h=h: e.tensor_tensor(scm[:, h % 2, :], ps[:, 0:P], masku[:], ALU.mult),
                         reads=[pk, "masku"], writes=[("scm", h % 2)])
                    pso, pko = nextps()

                    def f(e, pso=pso, h=h, tt=tt, tsl=tsl):
                        e.matmul(pso[:, :], scm[:, h % 2, :], vtk[:, tt, h * DV:(h + 1) * DV], start=True, stop=False)
                        ins = None
                        for dt in range(DKT):
                            ins = e.matmul(pso[:, :], qdT[:, h * DKT + dt, tsl], Sb[:, h * DKT + dt, :], start=False, stop=(dt == DKT - 1))
                        return ins
                    S.op("tensor", f, reads=[("scm", h % 2), ("v", tt), ("Sb", h)] + [("qd", h * DKT + dt) for dt in range(DKT)], writes=[pko])
                    for dt in range(DKT):
                        pss, pks = nextps()
                        S.op("tensor", lambda e, pss=pss, h=h, dt=dt, tt=tt: e.matmul(
                            pss[:, :], kst[:, tt, (h * DKT + dt) * P:(h * DKT + dt + 1) * P], vtk[:, tt, h * DV:(h + 1) * DV], start=True, stop=True),
                            reads=[("kst", tt), ("v", tt)], writes=[pks])
                        S.op("vector", lambda e, pss=pss, h=h, dt=dt, tt=tt: e.scalar_tensor_tensor(
                            St[:, h * DKT + dt, :], St[:, h * DKT + dt, :], dec[:, h * DKT + dt, tt:tt + 1], pss[:, :], ALU.mult, ALU.add),
                            reads=[pks, ("dec", h * DKT + dt), ("St", h, dt)], writes=[("St", h, dt)])
                    S.op("scalar", lambda e, pso=pso, h=h: e.activation(tmp[:, h % 2, :], pso[:, :], AF.Square, bias=c_zero, scale=1.0,
                                                                        accum_out=ssq[:, h:h + 1]),
                         reads=[pko, "col"], writes=[("tmp", h % 2), ("ssq", h)])
                    S.op("scalar", lambda e, h=h: e.activation(ssq[:, 4 + h:5 + h], ssq[:, h:h + 1], AF.Sqrt, bias=c_eps, scale=1.0 / DV),
                         reads=[("ssq", h), "col"], writes=[("ssq", 4 + h)])
                    S.op("vector", lambda e, h=h: e.reciprocal(ssq[:, 4 + h:5 + h], ssq[:, 4 + h:5 + h]),
                         reads=[("ssq", 4 + h)], writes=[("ssq", 4 + h)])
                    S.op("vector", lambda e, pso=pso, h=h: e.scalar_tensor_tensor(tmp[:, 2 + h % 2, :], pso[:, :], ssq[:, 4 + h:5 + h], gng[:], ALU.mult, ALU.mult),
                         reads=[pko, ("ssq", 4 + h), "gng"], writes=[("tmp", 2 + h % 2)])
                    S.op("vector", lambda e, h=h, tt=tt: e.tensor_tensor(tmpb[:, h % 2, :], tmp[:, 2 + h % 2, :], sgt[:, tt, h * DV:(h + 1) * DV], ALU.mult),
                         reads=[("tmp", 2 + h % 2), ("sg", tt)], writes=[("tmpb", h % 2)])
                    def f(e, h=h):
                        ins = None
                        for vt in range(VT):
                            ins = e.transpose(ptb[:, vt * P:(vt + 1) * P], tmpb[:, h % 2, vt * P:(vt + 1) * P], identb[:])
                        return ins
                    S.op("tensor", f, reads=[("tmpb", h % 2), "identb"], writes=[ptk])
                    for vt in range(VT):
                        S.op("scalar", lambda e, h=h, vt=vt, tsl=tsl: e.activation(
                            hnT[:, CT + h * VT + vt, tsl], ptb[:, vt * P:(vt + 1) * P], AF.Identity, bias=c_zero, scale=1.0),
                            reads=[ptk, "col"], writes=[("hn", CT + h * VT + vt)])
                for h in range(H):
                    S.op("scalar", lambda e, h=h: e.activation(Sb[:, h * DKT:(h + 1) * DKT, :], St[:, h * DKT:(h + 1) * DKT, :], AF.Identity, bias=c_zero, scale=1.0),
                         reads=[("St", h, dt) for dt in range(DKT)] + ["col"], writes=[("Sb", h)])

            def ev_o(j, ps, pk):
                S.op("vector", lambda e: e.scalar_tensor_tensor(x_sb[:, j, :], ps[:, 0:TS], mcol(2, j), x_sb[:, j, :], ALU.mult, ALU.add),
                     reads=[pk, "modl", ("x", j)], writes=[("x", j)])
            gemm_fm([(wouth[j, :, :], 1) for j in range(KT)], hnT, [("hn", kt) for kt in range(MT)], MT, ev_o)

            norm_to_hnT(A2, 3)
            for fb in range(NFB):
                def ev_h(j, ps, pk, fb=fb):
                    S.op("scalar", lambda e: e.activation(tmp[:, j % 2, 0:TS], ps[:, 0:TS], AF.Relu, bias=c_zero, scale=1.0),
                         reads=[pk, "col"], writes=[("tmp", j % 2)])
                    S.op("vector", lambda e: e.tensor_tensor(h1T[:, j, :], tmp[:, j % 2, 0:TS], tmp[:, j % 2, 0:TS], ALU.mult),
                         reads=[("tmp", j % 2)], writes=[("h1", j)])
                gemm_fm([(wm1h[fb * FBT + j, :, :], 1) for j in range(FBT)], hnT, HN_ALL, KT, ev_h)

                def ev_f(j, ps, pk):
                    S.op("vector", lambda e: e.scalar_tensor_tensor(x_sb[:, j, :], ps[:, 0:TS], mcol(5, j), x_sb[:, j, :], ALU.mult, ALU.add),
                         reads=[pk, "modl", ("x", j)], writes=[("x", j)])
                gemm_fm([(wm2h[fb, jb, :, :], 4) for jb in range(D // 512)], h1T, [("h1", j) for j in range(FBT)], FBT, ev_f)

            if not last:
                S.op("sync", lambda e, t0=t0: e.dma_start(
                    out=xs[:, t0:t0 + TS].rearrange("(kt p) t -> p kt t", p=P), in_=x_sb[:]),
                    reads=[("x", kt) for kt in range(KT)], writes=["xdram"], dma_sem="dxo")
            else:
                ps, pk = nextps()
                for kt in (range(KT) if FINAL else ()):
                    S.op("scalar", lambda e, kt=kt: e.activation(tmp[:, kt % 2, 0:TS], x_sb[:, kt, :], AF.Square, bias=c_zero, scale=1.0),
                         reads=[("x", kt), "col"], writes=[("tmp", kt % 2)])
                    S.op("tensor", lambda e, kt=kt, ps=ps: e.matmul(ps[:, 0:TS], ones_f, tmp[:, kt % 2, 0:TS], start=(kt == 0), stop=(kt == KT - 1)),
                         reads=[("tmp", kt % 2), "cs"], writes=[pk])
                if FINAL:
                    S.op("scalar", lambda e, ps=ps: e.activation(rs[:, 0, :], ps[:, 0:TS], AF.Sqrt, bias=c_eps, scale=1.0 / D), reads=[pk, "col"], writes=[("rs", 0)])
                    S.op("vector", lambda e: e.reciprocal(rs[:, 0, :], rs[:, 0, :]), reads=[("rs", 0)], writes=[("rs", 0)])
                for kt in (range(KT) if FINAL else ()):
                    S.op("vector", lambda e, kt=kt: e.scalar_tensor_tensor(x_sb[:, kt, :], x_sb[:, kt, :], fgt[:, kt:kt + 1], rs[:, 0, :], ALU.mult, ALU.mult),
                         reads=[("x", kt), ("rs", 0), "fgt"], writes=[("x", kt)])
                S.op("sync", lambda e, t0=t0: e.dma_start(
                    out=outT[:, t0:t0 + TS].rearrange("(kt p) t -> p kt t", p=P), in_=x_sb[:]),
                    reads=[("x", kt) for kt in range(KT)], writes=["odram"], dma_sem="dxo")

    S.final_wait("sync", ["dxo"])

    semnames = sorted(S.cnt.keys())
    sem_ctx = [nc.semaphore(n) for n in semnames]
    sems = {n: c.__enter__() for n, c in zip(semnames, sem_ctx)}
    bounds = [{e: 0 for e in S.ENGS}] + S.marks + [{e: len(S.streams[e]) for e in S.ENGS}]
    for lo, hi in zip(bounds[:-1], bounds[1:]):
        with nc.Block() as block:
            def replay(engname, lo=lo, hi=hi):
                def run(eng):
                    for wl, fn, sem, inc in S.streams[engname][lo[engname]:hi[engname]]:
                        for s, v in wl:
                            eng.wait_ge(sems[s], v)
                        if fn is not None:
                            fn(eng).then_inc(sems[sem], inc)
                return run
            block.tensor(replay("tensor"))
            block.vector(replay("vector"))
            block.scalar(replay("scalar"))
            block.gpsimd(replay("gpsimd"))
            block.sync(replay("sync"))
    for c in reversed(sem_ctx):
        c.__exit__(None, None, None)
    for t in reversed(ctx):
        t.__exit__(None, None, None)
    return nc


def make_consts():
    m = np.arange(P)[:, None]
    l = np.arange(P)[None, :]
    ones = np.ones((P, P), np.float32)
    tri = np.where(m <= l, -1.0 / TAU, 0.0).astype(np.float32)
    sutri = np.where(m > l, -1.0 / TAU, 0.0).astype(np.float32)
    ident = np.eye(P, dtype=np.float32)
    masku = np.where(m <= l, 1.0, 0.0).astype(np.float32)
    return np.ascontiguousarray(np.concatenate([ones, tri, sutri, ident, masku], axis=1))


def colT(v, nt):
    return np.ascontiguousarray(np.asarray(v, np.float32).reshape(nt, P).T)


def prep_shared(cfg, inp):
    DEPTH, D, CC, H, DK, DV = (cfg[k] for k in ("DEPTH", "D", "CC", "H", "DK", "DV"))
    KT, CT, QW = D // P, CC // P, H * DK
    sh = {}
    def blk(W, ncol):
        W = np.asarray(W, np.float32)
        K_, N_ = W.shape
        return np.ascontiguousarray(W.reshape(K_ // P, P, N_ // ncol, ncol).transpose(2, 1, 0, 3)).reshape(N_ // ncol, P, (K_ // P) * ncol)
    sh["wadab"] = blk(inp["w_ada"], P)
    sh["b_adaT"] = colT(inp["b_ada"], 6 * KT)
    sh["modtabT"] = np.stack([colT(inp["mod_table"][l].reshape(-1), 6 * KT) for l in range(DEPTH)])
    sh["n1gT"] = np.stack([colT(inp["norm1_g"][l], KT) for l in range(DEPTH)])
    sh["n2gT"] = np.stack([colT(inp["norm2_g"][l], KT) for l in range(DEPTH)])
    sh["fgT"] = colT(inp["final_g"], KT)
    LR, FB = cfg["LR"], cfg["FB"]
    ncl = 2 * CC + 2 * QW + 2 * H * DV
    sh["winb"] = np.stack([blk(inp["w_in"][l][:, 0:ncl], P) for l in range(DEPTH)])
    sh["wlr"] = np.stack([blk(inp["w_in"][l][:, ncl:ncl + LR], LR)[0] for l in range(DEPTH)])
    sh["woutb"] = np.stack([blk(inp["w_out"][l], P) for l in range(DEPTH)])
    sh["wm1b"] = np.stack([blk(inp["w_mlp1"][l], P) for l in range(DEPTH)])
    DFF = np.asarray(inp["w_mlp2"]).shape[1]
    sh["wm2b"] = np.stack([np.stack([blk(np.asarray(inp["w_mlp2"][l])[fb * FB:(fb + 1) * FB, :], 512) for fb in range(DFF // FB)])
                           for l in range(DEPTH)])
    cwl = []
    for l in range(DEPTH):
        cwt = np.asarray(inp["conv_w"][l], np.float32).T.reshape(CT, P, CONV_W)
        cwl.append(np.ascontiguousarray(cwt.transpose(1, 0, 2).reshape(P, CT * CONV_W)))
    sh["convwT"] = np.stack(cwl)
    sh["convbT"] = np.stack([colT(inp["conv_b"][l], CT) for l in range(DEPTH)])
    sh["lngT"] = np.stack([colT(inp["conv_ln_g"][l], CT) for l in range(DEPTH)])
    sh["lnbT"] = np.stack([colT(inp["conv_ln_b"][l], CT) for l in range(DEPTH)])
    wa = np.zeros((DEPTH, 32, QW), np.float32)
    wa[:, 0:cfg["LR"], :] = inp["w_alpha"]
    wa[:, cfg["LR"], :] = inp["b_alpha"]
    sh["walpha"] = wa
    sh["gngbc"] = np.ascontiguousarray(np.broadcast_to(np.asarray(inp["gla_norm_g"], np.float32)[:, None, :], (DEPTH, P, DV)))
    sh["cst"] = make_consts()
    return sh


def run_cfg(cfg, inp, trace=False):
    nc = build_program(cfg)
    sh = prep_shared(cfg, inp)
    KT = cfg["D"] // P
    ncores = cfg["NCORES"]
    in_maps = []
    for b in range(ncores):
        m = dict(sh)
        m["xT"] = np.ascontiguousarray(np.asarray(inp["x"][b], np.float32).T)
        m["scT"] = colT(inp["c"][b], KT)
        in_maps.append(m)
    res = run_bass_kernel_spmd(nc, in_maps, core_ids=list(range(ncores)), trace=trace)
    out = np.stack([np.ascontiguousarray(res.results[b]["outT"].T) for b in range(ncores)])
    return out, res


PER_LAYER = ("mod_table", "norm1_g", "w_in", "conv_w", "conv_b", "conv_ln_g", "conv_ln_b", "w_alpha",
             "b_alpha", "gla_norm_g", "w_out", "norm2_g", "w_mlp1", "w_mlp2")
_PROGS = {}


def run_layers(cfg, inp, xT_list, l0):
    key = (cfg["DEPTH"], cfg["NT"], cfg.get("FINAL", True), cfg["NCORES"])
    if key not in _PROGS:
        _PROGS[key] = build_program(cfg)
    nc = _PROGS[key]
    sub = {k: (np.asarray(v)[l0:l0 + cfg["DEPTH"]] if k in PER_LAYER else v) for k, v in inp.items()}
    sh = prep_shared(cfg, sub)
    KT = cfg["D"] // P
    in_maps = []
    for b in range(cfg["NCORES"]):
        m = dict(sh)
        m["xT"] = xT_list[b]
        m["scT"] = colT(inp["c"][b], KT)
        in_maps.append(m)
    res = run_bass_kernel_spmd(nc, in_maps, core_ids=list(range(cfg["NCORES"])))
    return [np.ascontiguousarray(res.results[b]["outT"]) for b in range(cfg["NCORES"])]


LAYERS_PER_LAUNCH = 1


def kernel(**inputs):
    B = FULL_CFG["NCORES"]
    cur = [np.ascontiguousarray(np.asarray(inputs["x"][b], np.float32).T) for b in range(B)]
    depth = FULL_CFG["DEPTH"]
    for l0 in range(0, depth, LAYERS_PER_LAUNCH):
        cfg = dict(FULL_CFG, DEPTH=LAYERS_PER_LAUNCH, FINAL=(l0 + LAYERS_PER_LAUNCH >= depth))
        cur = run_layers(cfg, inputs, cur, l0)
    return np.stack([np.ascontiguousarray(c.T) for c in cur]).astype(np.float32)
```
